# Optimizing a Trainium2 kernel written in Bass

```python
import jax, jax.numpy as jnp
from jax import lax
import numpy as np

D_MODEL = 2048
BATCH = 8
SEQ = 2048
DEPTH = 2

HEAD_DIM = 128
QBLOCK = 128
GRID_W = 64
EPS = 1e-6
NEG_INF = -1e30

MLA_HEADS = 8
MLA_Q_RANK = 512
MLA_KV_RANK = 256
MLA_NOPE = 128
MLA_ROPE = 64
MLA_V = 128
MLA_ROPE_THETA = 10000.0

GQA_HEADS = 8
GQA_KV_HEADS = 2
GQA_GROUP = GQA_HEADS // GQA_KV_HEADS
AXIAL_THETA = 10000.0

DIL_PATTERNS = ((128, 1), (512, 4), (2048, 16))
DIL_HEADS_PER_GROUP = 4
DIL_HEADS = DIL_HEADS_PER_GROUP * len(DIL_PATTERNS)
PARTIAL_ROPE_DIM = HEAD_DIM // 4
PARTIAL_ROPE_THETA = 500000.0

N_BRANCHES = 3
D_FF = 4 * D_MODEL

A_COLS = MLA_Q_RANK + MLA_KV_RANK + MLA_ROPE
B_COLS = (GQA_HEADS + 2 * GQA_KV_HEADS) * HEAD_DIM
C_COLS = 3 * DIL_HEADS * HEAD_DIM
GATE_COLS = N_BRANCHES * D_MODEL
IN_COLS = A_COLS + B_COLS + C_COLS + GATE_COLS

kernel_name = "hybrid_gated_mla_axialgqa_dilated_encoder"


def rms_norm(x, gain):
    xf = x.astype(jnp.float32)
    y = xf * lax.rsqrt(jnp.mean(xf * xf, axis=-1, keepdims=True) + EPS)
    return (y * gain.astype(jnp.float32)).astype(x.dtype)


def rotary(x, pos, theta):
    half = x.shape[-1] // 2
    inv = theta ** (-jnp.arange(half, dtype=jnp.float32) / half)
    ang = pos[:, None] * inv[None, :]
    cos = jnp.cos(ang)[:, None, :]
    sin = jnp.sin(ang)[:, None, :]
    xf = x.astype(jnp.float32)
    x1, x2 = xf[..., :half], xf[..., half:]
    return jnp.concatenate([x1 * cos - x2 * sin, x1 * sin + x2 * cos], axis=-1).astype(x.dtype)


def dense_block_attention(q, k, v):
    b, s, hkv, g, dk = q.shape
    scale = dk ** -0.5
    nb = s // QBLOCK
    qb = jnp.moveaxis(q.reshape(b, nb, QBLOCK, hkv, g, dk), 1, 0)

    def attend(qblk):
        sc = jnp.einsum('bqhgd,bkhd->bhgqk', qblk, k).astype(jnp.float32) * scale
        p = jax.nn.softmax(sc, axis=-1).astype(v.dtype)
        return jnp.einsum('bhgqk,bkhd->bqhgd', p, v)

    o = lax.map(attend, qb)
    return jnp.moveaxis(o, 0, 1).reshape(b, s, hkv * g, v.shape[-1])


def mla_mixer(xa, pos, q_lat_norm, w_uq, kv_lat_norm, w_ukv, q_head_norm, k_head_norm):
    b, s, _ = xa.shape
    c_q = xa[..., :MLA_Q_RANK]
    c_kv = xa[..., MLA_Q_RANK:MLA_Q_RANK + MLA_KV_RANK]
    k_pe = xa[..., MLA_Q_RANK + MLA_KV_RANK:]
    q = (rms_norm(c_q, q_lat_norm) @ w_uq).reshape(b, s, MLA_HEADS, MLA_NOPE + MLA_ROPE)
    kv = (rms_norm(c_kv, kv_lat_norm) @ w_ukv).reshape(b, s, MLA_HEADS, MLA_NOPE + MLA_V)
    k_nope, v = kv[..., :MLA_NOPE], kv[..., MLA_NOPE:]
    k = jnp.concatenate(
        [k_nope, jnp.broadcast_to(k_pe[:, :, None, :], (b, s, MLA_HEADS, MLA_ROPE))], axis=-1)
    q = rms_norm(q, q_head_norm)
    k = rms_norm(k, k_head_norm)
    q = jnp.concatenate([q[..., :MLA_NOPE], rotary(q[..., MLA_NOPE:], pos, MLA_ROPE_THETA)], axis=-1)
    k = jnp.concatenate([k[..., :MLA_NOPE], rotary(k[..., MLA_NOPE:], pos, MLA_ROPE_THETA)], axis=-1)
    o = dense_block_attention(q[:, :, :, None, :], k, v)
    return o.reshape(b, s, MLA_HEADS * MLA_V)


def axial_rotary(x, row, col):
    half = x.shape[-1] // 2
    return jnp.concatenate([rotary(x[..., :half], row, AXIAL_THETA),
                            rotary(x[..., half:], col, AXIAL_THETA)], axis=-1)


def gqa_mixer(xb, row, col, q_norm, k_norm):
    b, s, _ = xb.shape
    nq = GQA_HEADS * HEAD_DIM
    nk = GQA_KV_HEADS * HEAD_DIM
    q = xb[..., :nq].reshape(b, s, GQA_HEADS, HEAD_DIM)
    k = xb[..., nq:nq + nk].reshape(b, s, GQA_KV_HEADS, HEAD_DIM)
    v = xb[..., nq + nk:].reshape(b, s, GQA_KV_HEADS, HEAD_DIM)
    q = axial_rotary(rms_norm(q, q_norm), row, col)
    k = axial_rotary(rms_norm(k, k_norm), row, col)
    q = q.reshape(b, s, GQA_KV_HEADS, GQA_GROUP, HEAD_DIM)
    o = dense_block_attention(q, k, v)
    return o.reshape(b, s, GQA_HEADS * HEAD_DIM)


def dilated_window_attention(q, k, v, dilation, radius):
    b, s, h, d = q.shape
    scale = d ** -0.5
    L = s // dilation

    def strided(t):
        return t.reshape(b, L, dilation, h, d).transpose(0, 2, 1, 3, 4)

    qd, kd, vd = strided(q), strided(k), strided(v)
    nb = -(-L // QBLOCK)
    lq = nb * QBLOCK
    kb_len = QBLOCK + 2 * radius
    qp = jnp.pad(qd, ((0, 0), (0, 0), (0, lq - L), (0, 0), (0, 0)))
    pad_kv = ((0, 0), (0, 0), (radius, lq - L + radius), (0, 0), (0, 0))
    kp = jnp.pad(kd, pad_kv)
    vp = jnp.pad(vd, pad_kv)
    idx = jnp.arange(nb)[:, None] * QBLOCK + jnp.arange(kb_len)[None, :]
    kb = kp[:, :, idx]
    vb = vp[:, :, idx]
    qb = qp.reshape(b, dilation, nb, QBLOCK, h, d)
    sc = jnp.einsum('brnqhd,brnkhd->brnhqk', qb, kb).astype(jnp.float32) * scale
    rel = jnp.arange(kb_len)[None, :] - radius - jnp.arange(QBLOCK)[:, None]
    korig = idx - radius
    valid = (jnp.abs(rel) <= radius)[None] & ((korig >= 0) & (korig < L))[:, None, :]
    sc = jnp.where(valid[:, None], sc, NEG_INF)
    lse = jax.nn.logsumexp(sc, axis=-1)
    p = jnp.exp(sc - lse[..., None]).astype(v.dtype)
    o = jnp.einsum('brnhqk,brnkhd->brnqhd', p, vb)
    o = o.reshape(b, dilation, lq, h, d)[:, :, :L].transpose(0, 2, 1, 3, 4).reshape(b, s, h, d)
    lse = lse.transpose(0, 1, 2, 4, 3).reshape(b, dilation, lq, h)[:, :, :L]
    lse = lse.transpose(0, 2, 1, 3).reshape(b, s, h)
    return o, lse


def dilated_mixer(xc, pos, q_norm, k_norm):
    b, s, _ = xc.shape
    w = DIL_HEADS * HEAD_DIM
    q = xc[..., :w].reshape(b, s, DIL_HEADS, HEAD_DIM)
    k = xc[..., w:2 * w].reshape(b, s, DIL_HEADS, HEAD_DIM)
    v = xc[..., 2 * w:].reshape(b, s, DIL_HEADS, HEAD_DIM)
    q = rms_norm(q, q_norm)
    k = rms_norm(k, k_norm)
    q = jnp.concatenate([rotary(q[..., :PARTIAL_ROPE_DIM], pos, PARTIAL_ROPE_THETA),
                         q[..., PARTIAL_ROPE_DIM:]], axis=-1)
    k = jnp.concatenate([rotary(k[..., :PARTIAL_ROPE_DIM], pos, PARTIAL_ROPE_THETA),
                         k[..., PARTIAL_ROPE_DIM:]], axis=-1)
    outs, lses = [], []
    for gi, (window, dilation) in enumerate(DIL_PATTERNS):
        hs = slice(gi * DIL_HEADS_PER_GROUP, (gi + 1) * DIL_HEADS_PER_GROUP)
        o, lse = dilated_window_attention(q[:, :, hs], k[:, :, hs], v[:, :, hs],
                                          dilation, window // (2 * dilation))
        outs.append(o)
        lses.append(lse)
    wts = jax.nn.softmax(jnp.stack(lses, axis=0), axis=0)
    o = jnp.sum(wts[..., None] * jnp.stack(outs, axis=0).astype(jnp.float32), axis=0)
    return o.astype(xc.dtype).reshape(b, s, DIL_HEADS_PER_GROUP * HEAD_DIM)


def _normal(key, shape, scale):
    return scale * jax.random.normal(key, shape, jnp.float32)


def _gain(key, shape):
    return 1.0 + 0.05 * jax.random.normal(key, shape, jnp.float32)


def setup_inputs(seed: int = 0) -> dict:
    key = jax.random.key(seed)
    ks = jax.random.split(key, 24)
    return {
        "x": _normal(ks[0], (BATCH, SEQ, D_MODEL), 1.0),
        "attn_norm": _gain(ks[1], (DEPTH, D_MODEL)),
        "w_in": _normal(ks[2], (DEPTH, D_MODEL, IN_COLS), D_MODEL ** -0.5),
        "b_gate": _normal(ks[3], (DEPTH, GATE_COLS), 0.01),
        "mla_q_lat_norm": _gain(ks[4], (DEPTH, MLA_Q_RANK)),
        "w_uq": _normal(ks[5], (DEPTH, MLA_Q_RANK, MLA_HEADS * (MLA_NOPE + MLA_ROPE)), MLA_Q_RANK ** -0.5),
        "mla_kv_lat_norm": _gain(ks[6], (DEPTH, MLA_KV_RANK)),
        "w_ukv": _normal(ks[7], (DEPTH, MLA_KV_RANK, MLA_HEADS * (MLA_NOPE + MLA_V)), MLA_KV_RANK ** -0.5),
        "mla_q_head_norm": _gain(ks[8], (DEPTH, MLA_NOPE + MLA_ROPE)),
        "mla_k_head_norm": _gain(ks[9], (DEPTH, MLA_NOPE + MLA_ROPE)),
        "gqa_q_norm": _gain(ks[10], (DEPTH, HEAD_DIM)),
        "gqa_k_norm": _gain(ks[11], (DEPTH, HEAD_DIM)),
        "dil_q_norm": _gain(ks[12], (DEPTH, HEAD_DIM)),
        "dil_k_norm": _gain(ks[13], (DEPTH, HEAD_DIM)),
        "w_oa": _normal(ks[14], (DEPTH, MLA_HEADS * MLA_V, D_MODEL), (MLA_HEADS * MLA_V) ** -0.5),
        "w_ob": _normal(ks[15], (DEPTH, GQA_HEADS * HEAD_DIM, D_MODEL), (GQA_HEADS * HEAD_DIM) ** -0.5),
        "w_oc": _normal(ks[16], (DEPTH, DIL_HEADS_PER_GROUP * HEAD_DIM, D_MODEL),
                        (DIL_HEADS_PER_GROUP * HEAD_DIM) ** -0.5),
        "w_out": _normal(ks[17], (DEPTH, D_MODEL, D_MODEL), D_MODEL ** -0.5),
        "mlp_norm": _gain(ks[18], (DEPTH, D_MODEL)),
        "w_up": _normal(ks[19], (DEPTH, D_MODEL, D_FF), D_MODEL ** -0.5),
        "w_down": _normal(ks[20], (DEPTH, D_FF, D_MODEL), D_FF ** -0.5),
    }


def reference(x, attn_norm, w_in, b_gate, mla_q_lat_norm, w_uq, mla_kv_lat_norm, w_ukv,
              mla_q_head_norm, mla_k_head_norm, gqa_q_norm, gqa_k_norm, dil_q_norm, dil_k_norm,
              w_oa, w_ob, w_oc, w_out, mlp_norm, w_up, w_down):
    b, s, _ = x.shape
    rows = s // GRID_W
    pos = jnp.arange(s, dtype=jnp.float32)
    row = jnp.repeat(jnp.arange(rows, dtype=jnp.float32), GRID_W)
    col = jnp.tile(jnp.arange(GRID_W, dtype=jnp.float32), rows)
    for l in range(DEPTH):
        xn = rms_norm(x, attn_norm[l])
        proj = xn @ w_in[l]
        xa = proj[..., :A_COLS]
        xb = proj[..., A_COLS:A_COLS + B_COLS]
        xc = proj[..., A_COLS + B_COLS:A_COLS + B_COLS + C_COLS]
        gl = proj[..., A_COLS + B_COLS + C_COLS:] + b_gate[l]
        ya = mla_mixer(xa, pos, mla_q_lat_norm[l], w_uq[l], mla_kv_lat_norm[l], w_ukv[l],
                       mla_q_head_norm[l], mla_k_head_norm[l]) @ w_oa[l]
        yb = gqa_mixer(xb, row, col, gqa_q_norm[l], gqa_k_norm[l]) @ w_ob[l]
        yc = dilated_mixer(xc, pos, dil_q_norm[l], dil_k_norm[l]) @ w_oc[l]
        gates = jax.nn.sigmoid(gl.astype(jnp.float32)).astype(x.dtype).reshape(b, s, N_BRANCHES, D_MODEL)
        merged = gates[:, :, 0] * ya + gates[:, :, 1] * yb + gates[:, :, 2] * yc
        x = x + merged @ w_out[l]
        hn = rms_norm(x, mlp_norm[l])
        x = x + jnp.square(jax.nn.relu(hn @ w_up[l])) @ w_down[l]
    return x
```

```python
from contextlib import ExitStack
import numpy as np
import concourse.bass as bass
import concourse.mybir as mybir
from concourse.bass_utils import run_bass_kernel_spmd

F32 = mybir.dt.float32
BF16 = mybir.dt.bfloat16
AF = mybir.ActivationFunctionType
ALU = mybir.AluOpType
AX = mybir.AxisListType

SEG = 30000
NDMASEM = 24
EPS = 1e-6
NTOK = 2048
DM = 2048
NT = 16
DEPTH = 2
DFF = 8192

CQ0, CKV0, KPE0 = 0, 512, 768
BQ0, BK0, BV0 = 832, 1856, 2112
CQD0, CKD0, CVD0 = 2368, 3904, 5440
G0 = 6976
DILS = (1, 4, 16)


class Buf:
    __slots__ = ("name", "lw", "rd")

    def __init__(self, name=""):
        self.name = name
        self.lw = None
        self.rd = {}


class Op:
    __slots__ = ("eng", "idx", "fn", "waits", "signal", "sig", "is_dma", "dsem", "dval")

    def __init__(self, eng, idx, fn, is_dma):
        self.eng = eng
        self.idx = idx
        self.fn = fn
        self.waits = []
        self.signal = False
        self.sig = None
        self.is_dma = is_dma
        self.dsem = None
        self.dval = 0


class Sched:
    ENGS = ("pe", "act", "dve", "pool", "sp")

    def __init__(self, nc, same_engine_sync=True):
        self.nc = nc
        self.ops = {e: [] for e in self.ENGS}
        self.same = same_engine_sync
        self.seen = {e: {} for e in self.ENGS}
        self.ndma = {e: 0 for e in self.ENGS}
        self.dma_hist = {e: [] for e in self.ENGS}
        self.last_compute = {e: None for e in self.ENGS}

    def _dep(self, op, dep):
        if dep is None:
            return
        if dep.is_dma:
            key = ("d", dep.eng, dep.dsem)
            val = dep.dval
        else:
            if dep.eng == op.eng and (dep.eng == "pe" or not self.same):
                return
            key = ("e", dep.eng)
            val = dep.idx
        seen = self.seen[op.eng]
        if seen.get(key, -1) >= val:
            return
        seen[key] = val
        op.waits.append(dep)
        dep.signal = True

    def op(self, eng, fn, reads=(), writes=(), dma=False):
        lst = self.ops[eng]
        o = Op(eng, len(lst), fn, dma)
        if dma:
            n = self.ndma[eng]
            self.ndma[eng] = n + 1
            o.dsem = n % NDMASEM
            o.dval = 16 * (n // NDMASEM + 1)
            hist = self.dma_hist[eng]
            if n >= NDMASEM:
                self._dep(o, hist[n - NDMASEM])
            hist.append(o)
        else:
            self.last_compute[eng] = o
        for b in reads:
            self._dep(o, b.lw)
        for b in writes:
            self._dep(o, b.lw)
            for r in b.rd.values():
                self._dep(o, r)
        for b in reads:
            b.rd[("dma", eng, o.idx) if dma else eng] = o
        for b in writes:
            b.lw = o
            b.rd = {}
        lst.append(o)
        return o

    def barrier(self):
        deps = []
        for e in self.ENGS:
            if self.last_compute[e] is not None:
                deps.append(self.last_compute[e])
            deps.extend(self.dma_hist[e][-NDMASEM:])
        for e in self.ENGS:
            if e == "sp" or self.ops[e]:
                o = Op(e, len(self.ops[e]), lambda eng: eng.nop(), False)
                for d in deps:
                    if d.eng == e and not d.is_dma:
                        continue
                    self._dep(o, d)
                self.ops[e].append(o)

    def emit(self):
        nc = self.nc
        esems = {}
        for e in self.ENGS:
            cnt = 0
            for o in self.ops[e]:
                if o.is_dma:
                    continue
                if o.signal:
                    o.sig = (cnt // SEG, cnt % SEG + 1)
                    cnt += 1
            nseg = (cnt + SEG - 1) // SEG
            esems[e] = [nc.alloc_semaphore(f"s_{e}_{i}") for i in range(max(nseg, 1))]
        dsems = {e: [nc.alloc_semaphore(f"d_{e}_{i}") for i in range(NDMASEM)]
                 for e in self.ENGS if self.ndma[e] > 0}
        engobj = {"pe": "tensor", "act": "scalar", "dve": "vector", "pool": "gpsimd", "sp": "sync"}

        def run(e, eng):
            for o in self.ops[e]:
                for d in o.waits:
                    if d.is_dma:
                        eng.wait_ge(dsems[d.eng][d.dsem], d.dval)
                    else:
                        eng.wait_ge(esems[d.eng][d.sig[0]], d.sig[1])
                ins = o.fn(eng)
                if o.is_dma:
                    ins.then_inc(dsems[e][o.dsem], 16)
                elif o.signal:
                    ins.then_inc(esems[e][o.sig[0]], 1)
            for o in self.dma_hist[e][-NDMASEM:]:
                eng.wait_ge(dsems[e][o.dsem], o.dval)

        with nc.Block() as block:
            for e in self.ENGS:
                if not self.ops[e]:
                    continue
                getattr(block, engobj[e])(lambda eng, e=e: run(e, eng))
        return {e: len(self.ops[e]) for e in self.ENGS}


def _rot_tab(pos, half, theta):
    inv = (np.float32(theta) ** (-np.arange(half, dtype=np.float32) / np.float32(half))).astype(np.float32)
    ang = (pos.astype(np.float32)[:, None] * inv[None, :]).astype(np.float32)
    return np.cos(ang).astype(np.float32), np.sin(ang).astype(np.float32)


def make_tables():
    t = np.arange(NTOK)
    c, s = _rot_tab(t, 32, 10000.0)
    tab_mla = np.concatenate([c, s], axis=1)
    cr, sr = _rot_tab(t // 64, 32, 10000.0)
    cc, sc = _rot_tab(t % 64, 32, 10000.0)
    tab_gqa = np.concatenate([cr, sr, cc, sc], axis=1)
    tabs = []
    for dil in DILS:
        L = NTOK // dil
        i = np.arange(NTOK)
        tok = (i % L) * dil + (i // L)
        c, s = _rot_tab(tok, 16, 500000.0)
        tabs.append(np.concatenate([c, s], axis=1))
    tab_dil = np.stack(tabs, 0)
    p = np.arange(128)[:, None]
    f = np.arange(128)[None, :]
    m1 = (p >= f)
    m2 = (p <= f)
    mf = (p < 64) & (p >= f - 64)
    ml = (p >= 64) & (p <= f + 64)
    mb = (np.abs(p - f) <= 64)
    masks = np.stack([m1, m2, mf, ml, mb], 0).astype(np.float32)
    return dict(tab_mla=tab_mla.astype(np.float32), tab_gqa=tab_gqa.astype(np.float32),
                tab_dil=tab_dil.astype(np.float32), masks=masks)


WSHAPES = dict(
    attn_norm=[DEPTH, DM], w_in=[DEPTH, DM, 13120], b_gate=[DEPTH, 6144],
    mla_q_lat_norm=[DEPTH, 512], w_uq=[DEPTH, 512, 1536], mla_kv_lat_norm=[DEPTH, 256],
    w_ukv=[DEPTH, 256, 2048], mla_q_head_norm=[DEPTH, 192], mla_k_head_norm=[DEPTH, 192],
    gqa_q_norm=[DEPTH, 128], gqa_k_norm=[DEPTH, 128], dil_q_norm=[DEPTH, 128], dil_k_norm=[DEPTH, 128],
    w_oa=[DEPTH, 1024, DM], w_ob=[DEPTH, 1024, DM], w_oc=[DEPTH, 512, DM], w_out=[DEPTH, DM, DM],
    mlp_norm=[DEPTH, DM], w_up=[DEPTH, DM, DFF], w_down=[DEPTH, DFF, DM],
    tab_mla=[NTOK, 64], tab_gqa=[NTOK, 128], tab_dil=[3, NTOK, 32], masks=[5, 128, 128],
)


class Builder:
    def __init__(self, stop_after=None, debug=False, nlayers=DEPTH, wl=DEPTH, used=None, parts=None, dbg_names=()):
        self.nc = nc = bass.Bass("TRN2", target_bir_lowering=False)
        self.S = Sched(nc)
        self.stop_after = stop_after
        self.debug = debug
        self.nlayers = nlayers
        self.uid = 0
        self.x_in = nc.dram_tensor("x", [NTOK, DM], F32, kind="ExternalInput").ap()
        self.out = nc.dram_tensor("out", [NTOK, DM], F32, kind="ExternalOutput").ap()
        self.parts = parts
        self.W = {}
        for k, shp in WSHAPES.items():
            if used is not None and k not in used:
                continue
            shp = list(shp)
            if not k.startswith("tab_") and k != "masks":
                shp[0] = wl
            self.W[k] = nc.dram_tensor(k, shp, F32, kind="ExternalInput").ap()

        def scr(name, shape, dt):
            return nc.dram_tensor(name, shape, dt, kind=("ExternalOutput" if name in dbg_names else "Internal")).ap()
        self.xa = scr("xa", [NTOK, DM], F32)
        self.xb = scr("xb", [NTOK, DM], F32)
        self.qT_mla = scr("qT_mla", [8, 192, NTOK], BF16)
        self.kT_mla = scr("kT_mla", [8, 192, NTOK], BF16)
        self.v_mla = scr("v_mla", [NTOK, 1024], BF16)
        self.qT_g = scr("qT_g", [8, 128, NTOK], BF16)
        self.kT_g = scr("kT_g", [2, 128, NTOK], BF16)
        self.v_g = scr("v_g", [NTOK, 256], BF16)
        self.qT_d = scr("qT_d", [12, 128, NTOK], BF16)
        self.kT_d = scr("kT_d", [12, 128, NTOK], BF16)
        self.v_d = scr("v_d", [3, NTOK, 512], BF16)
        self.oT = scr("oT", [20, 128, NTOK], BF16)
        self.sigT = scr("sigT", [48, 128, NTOK], F32)
        if debug and "dbg_xt" in dbg_names:
            self.dbg_xt = nc.dram_tensor("dbg_xt", [128, 16 * NTOK], BF16, kind="ExternalOutput").ap()

    def name(self, p="t"):
        self.uid += 1
        return f"{p}{self.uid}"

    def sb(self, st, shape, dt, p="t"):
        return st.enter_context(self.nc.sbuf_tensor(self.name(p), shape, dt)).ap()

    def ps(self, st, shape, dt, p="p"):
        return st.enter_context(self.nc.psum_tensor(self.name(p), shape, dt)).ap()

    def dma(self, q, out, in_, reads=(), writes=()):
        return self.S.op(q, lambda e, o=out, i=in_: e.dma_start(out=o, in_=i), reads=reads, writes=writes, dma=True)

    def wload(self, dst, src, wbuf, nsplit=1):
        C = dst.shape[1]
        step = max(1, C // nsplit)
        for c0 in range(0, C, step):
            c1 = min(C, c0 + step)
            s = src[c0 * 128:c1 * 128, :].rearrange("(c p) n -> p c n", p=128)
            self.dma("pool", dst[:, c0:c1, :], s, writes=[wbuf])

    def setup_consts(self, st):
        nc, S = self.nc, self.S
        self.identf = self.sb(st, [128, 128], F32, "identf")
        self.ident = self.sb(st, [128, 128], BF16, "ident")
        self.ones = self.sb(st, [128, 128], BF16, "ones")
        self.maskf = self.sb(st, [128, 5, 128], F32, "maskf")
        self.mask = self.sb(st, [128, 5, 128], BF16, "mask")
        self.XT = self.sb(st, [128, 16, NTOK], BF16, "XT")
        self.XTB = [Buf(f"XT{i}") for i in range(NT)]
        cb = Buf("consts")
        self.cb = cb
        S.op("pool", lambda e: e.memset(self.identf, 1.0), writes=[cb])
        S.op("pool", lambda e: e.affine_select(out=self.identf, in_=self.identf, pattern=[[-1, 128]],
                                                compare_op=ALU.is_equal, fill=0.0, base=0, channel_multiplier=1),
             reads=[cb], writes=[cb])
        S.op("act", lambda e: e.copy(out=self.ident, in_=self.identf), reads=[cb], writes=[cb])
        S.op("pool", lambda e: e.memset(self.ones, 1.0), writes=[cb])
        self.dma("sp", self.maskf, self.W["masks"].rearrange("m p f -> p m f"), writes=[cb])
        S.op("act", lambda e: e.copy(out=self.mask, in_=self.maskf), reads=[cb], writes=[cb])

    def rstd(self, ss, tmp, n_dim, b):
        S = self.S
        S.op("dve", lambda e: e.tensor_scalar(out=tmp, in0=ss, scalar1=1.0 / n_dim, scalar2=EPS,
                                              op0=ALU.mult, op1=ALU.add), reads=[b], writes=[b])
        S.op("act", lambda e: e.activation(out=tmp, in_=tmp, func=AF.Sqrt), reads=[b], writes=[b])
        S.op("dve", lambda e: e.reciprocal(out=ss, in_=tmp), reads=[b], writes=[b])

    def transposes(self, src, srcb, n, dst_fn, dstbufs, pst, pstb, cnt, rows=128, width=128, src_fn=None, perm=None):
        S = self.S
        if src_fn is None:
            src_fn = lambda i: src[:, i * width:(i + 1) * width]
        i = 0
        while i < n:
            k = min(4, n - i)
            slot = cnt[0] % len(pst)
            cnt[0] += 1
            p, pb = pst[slot], pstb[slot]
            for j in range(k):
                S.op("pe", lambda e, p=p, j=j, a=src_fn(i + j):
                     e.transpose(p[0:rows, j, :], a, self.ident), reads=[srcb, self.cb], writes=[pb])
            if perm is not None:
                dil = perm
                for j in range(k):
                    d = dst_fn(i + j, 1)
                    sv = p[0:rows, j, :].rearrange("p (j r) -> p r j", r=dil)
                    if (cnt[0] + j) % 2 == 0:
                        S.op("act", lambda e, d=d, sv=sv: e.copy(out=d, in_=sv), reads=[pb], writes=dstbufs)
                    else:
                        S.op("dve", lambda e, d=d, sv=sv: e.tensor_copy(out=d, in_=sv), reads=[pb], writes=dstbufs)
                i += k
                continue
            d = dst_fn(i, k)
            if cnt[0] % 2 == 0:
                S.op("act", lambda e, d=d, p=p, k=k: e.copy(out=d, in_=p[0:rows, 0:k, :]), reads=[pb], writes=dstbufs)
            else:
                S.op("dve", lambda e, d=d, p=p, k=k: e.tensor_copy(out=d, in_=p[0:rows, 0:k, :]), reads=[pb], writes=dstbufs)
            i += k

    def rotary(self, x1, x2, cs, sn, t, rd, xb_, rtb):
        S = self.S
        S.op("dve", lambda e: e.tensor_tensor(out=t[0], in0=x1, in1=cs, op=ALU.mult), reads=rd, writes=[rtb])
        S.op("pool", lambda e: e.tensor_tensor(out=t[1], in0=x2, in1=sn, op=ALU.mult), reads=rd, writes=[rtb])
        S.op("dve", lambda e: e.tensor_tensor(out=t[2], in0=x1, in1=sn, op=ALU.mult), reads=rd, writes=[rtb])
        S.op("pool", lambda e: e.tensor_tensor(out=t[3], in0=x2, in1=cs, op=ALU.mult), reads=rd, writes=[rtb])
        S.op("dve", lambda e: e.tensor_tensor(out=x1, in0=t[0], in1=t[1], op=ALU.subtract), reads=[rtb], writes=[xb_])
        S.op("pool", lambda e: e.tensor_tensor(out=x2, in0=t[2], in1=t[3], op=ALU.add), reads=[rtb], writes=[xb_])

    def phase_normt(self, src, gain):
        S = self.S
        with ExitStack() as st:
            xt = [self.sb(st, [128, DM], F32) for _ in range(2)]
            xtb = [Buf() for _ in range(2)]
            xn = [self.sb(st, [128, DM], BF16) for _ in range(2)]
            xnb = [Buf() for _ in range(2)]
            junk = self.sb(st, [128, DM], BF16)
            jb = Buf()
            gbc = self.sb(st, [128, DM], F32)
            gb = Buf()
            stt = [self.sb(st, [128, 4], F32) for _ in range(2)]
            stb = [Buf() for _ in range(2)]
            pst = [self.ps(st, [128, 4, 128], BF16) for _ in range(4)]
            pstb = [Buf() for _ in range(4)]
            cnt = [0]
            self.dma("sp", gbc, gain.partition_broadcast(128), writes=[gb])
            for tt in range(NT):
                b = tt % 2
                self.dma("sp", xt[b], src[tt * 128:(tt + 1) * 128, :], writes=[xtb[b]])
                self.norm_tile(xt[b], xtb[b], xn[b], xnb[b], junk, jb, gbc, gb, stt[b], stb[b], DM)
                self.transposes(xn[b], xnb[b], 16,
                                lambda i, k, tt=tt: self.XT[:, i:i + k, tt * 128:(tt + 1) * 128],
                                [self.XTB[tt]], pst, pstb, cnt)
        S.barrier()

    def norm_tile(self, x, xb_, out, outb, junk, jb, gbc, gb, st4, stb, n):
        S = self.S
        S.op("act", lambda e: e.activation(out=junk[:, 0:n], in_=x, func=AF.Square, accum_out=st4[:, 0:1]),
             reads=[xb_], writes=[jb, stb])
        self.rstd(st4[:, 0:1], st4[:, 1:2], n, stb)
        S.op("dve", lambda e: e.scalar_tensor_tensor(out=out, in0=x, scalar=st4[:, 0:1], in1=gbc[:, 0:n],
                                                     op0=ALU.mult, op1=ALU.mult),
             reads=[xb_, stb, gb], writes=[outb])

    def tok_slice(self, dil, ct):
        if dil == 1:
            return slice(ct * 128, (ct + 1) * 128)
        nb = NT // dil
        r, b = ct // nb, ct % nb
        start = r + dil * 128 * b
        return slice(start, start + dil * 127 + 1, dil)

    def phase_proj(self, l):
        S, W = self.S, self.W
        win = W["w_in"][l]
        with ExitStack() as st:
            wsb = [self.sb(st, [128, 16, 512], BF16, "wsb") for _ in range(2)]
            wb = [Buf() for _ in range(2)]
            psm = [self.ps(st, [128, 512], F32) for _ in range(3)]
            psmb = [Buf() for _ in range(3)]
            pst = [self.ps(st, [128, 4, 128], BF16) for _ in range(4)]
            pstb = [Buf() for _ in range(4)]
            xf = [self.sb(st, [128, 512], F32, "xf") for _ in range(2)]
            xfb = [Buf() for _ in range(2)]
            sq = self.sb(st, [128, 512], F32, "sq")
            sqb = Buf()
            xbf = [self.sb(st, [128, 512], BF16, "xbf") for _ in range(2)]
            xbfb = [Buf() for _ in range(2)]
            rt = [self.sb(st, [128, 4, 32], F32, "rt") for _ in range(4)]
            rtb = Buf()
            st8 = [self.sb(st, [128, 8], F32) for _ in range(2)]
            st8b = [Buf() for _ in range(2)]
            gbc = self.sb(st, [128, 512], F32, "gbc")
            gb = Buf()
            trig = [self.sb(st, [128, 128], F32, "trig") for _ in range(2)]
            trigb = [Buf() for _ in range(2)]
            stage = [self.sb(st, [128, 4, NTOK], BF16, "stage") for _ in range(2)]
            stageb = [Buf() for _ in range(2)]
            sg = [self.sb(st, [128, NTOK], F32, "sg") for _ in range(2)]
            sgb = [Buf() for _ in range(2)]
            bias = self.sb(st, [128, 48], F32, "bias")
            biasr = self.sb(st, [48, 128], F32, "biasr")
            bb = Buf()
            cnt = [0]
            grp = [0]
            mmc = [0]
            tilec = [0]

            def load_w(c0, n):
                i = grp[0] % 2
                grp[0] += 1
                self.wload(wsb[i][:, :, 0:n], win[:, c0:c0 + n], wb[i], nsplit=4)
                return wsb[i], wb[i]

            def mm_tokmajor(wt, wtb, n, dil, ct):
                i = mmc[0] % 3
                mmc[0] += 1
                ts_ = self.tok_slice(dil, ct)
                rb = self.XTB if dil != 1 else [self.XTB[ct]]
                for dc in range(16):
                    S.op("pe", lambda e, i=i, dc=dc, ts_=ts_, wt=wt, n=n:
                         e.matmul(psm[i][:, 0:n], lhsT=self.XT[:, dc, ts_], rhs=wt[:, dc, 0:n],
                                  start=(dc == 0), stop=(dc == 15)),
                         reads=rb + [wtb], writes=[psmb[i]])
                return psm[i], psmb[i]

            P = self.parts
            for (c0, n, gname, nn, dstT, dstb) in ((CQ0, 512, "mla_q_lat_norm", 512, self.CQT, self.cqb),
                                                   (CKV0, 320, "mla_kv_lat_norm", 256, self.CKVT, self.ckvb)):
                if P is not None and "A" not in P:
                    continue
                wt, wtb = load_w(c0, n)
                self.dma("sp", gbc[:, 0:nn], W[gname][l].partition_broadcast(128), writes=[gb])
                for tt in range(NT):
                    p, pb = mm_tokmajor(wt, wtb, n, 1, tt)
                    k = tilec[0] % 2
                    tilec[0] += 1
                    S.op("act", lambda e, p=p, k=k, n=n: e.copy(out=xf[k][:, 0:n], in_=p[:, 0:n]),
                         reads=[pb], writes=[xfb[k]])
                    self.norm_tile(xf[k][:, 0:nn], xfb[k], xbf[k][:, 0:nn], xbfb[k], sq, sqb, gbc, gb,
                                   st8[k], st8b[k], nn)
                    if n == 320:
                        S.op("pool", lambda e, k=k, tt=tt: e.tensor_copy(out=self.KPE[:, tt, :], in_=xf[k][:, 256:320]),
                             reads=[xfb[k]], writes=[self.kpeb])
                    self.transposes(xbf[k], xbfb[k], nn // 128,
                                    lambda i, kk, tt=tt, dstT=dstT: dstT[:, i:i + kk, tt * 128:(tt + 1) * 128],
                                    [dstb], pst, pstb, cnt)

            def head_group(c0, nh, gname, dil, rot, trig_src, trig_n, dstT, h0):
                wt, wtb = load_w(c0, nh * 128)
                self.dma("sp", gbc[:, 0:128], W[gname][l].partition_broadcast(128), writes=[gb])
                si = grp[0] % 2
                stg, stgb = stage[si], stageb[si]
                for ct in range(NT):
                    p, pb = mm_tokmajor(wt, wtb, nh * 128, 1, ct)
                    k = tilec[0] % 2
                    tilec[0] += 1
                    n = nh * 128
                    self.dma("sp", trig[k][:, 0:trig_n], trig_src[ct * 128:(ct + 1) * 128, :], writes=[trigb[k]])
                    S.op("act", lambda e, p=p, k=k, n=n: e.copy(out=xf[k][:, 0:n], in_=p[:, 0:n]),
                         reads=[pb], writes=[xfb[k]])
                    x3 = xf[k][:, 0:n].rearrange("p (h d) -> p h d", d=128)
                    S.op("dve", lambda e, k=k, n=n: e.tensor_tensor(out=sq[:, 0:n], in0=xf[k][:, 0:n], in1=xf[k][:, 0:n],
                                                                   op=ALU.mult), reads=[xfb[k]], writes=[sqb])
                    S.op("dve", lambda e, k=k, n=n, nh=nh: e.tensor_reduce(
                        out=st8[k][:, 0:nh], in_=sq[:, 0:n].rearrange("p (h d) -> p h d", d=128), axis=AX.X, op=ALU.add),
                        reads=[sqb], writes=[st8b[k]])
                    self.rstd(st8[k][:, 0:nh], st8[k][:, 4:4 + nh], 128, st8b[k])
                    S.op("dve", lambda e, x3=x3, k=k, nh=nh: e.tensor_tensor(
                        out=x3, in0=x3, in1=st8[k][:, 0:nh].unsqueeze(2).to_broadcast([128, nh, 128]), op=ALU.mult),
                        reads=[xfb[k], st8b[k]], writes=[xfb[k]])
                    S.op("pool", lambda e, x3=x3, nh=nh: e.tensor_tensor(
                        out=x3, in0=x3, in1=gbc[:, 0:128].unsqueeze(1).to_broadcast([128, nh, 128]), op=ALU.mult),
                        reads=[xfb[k], gb], writes=[xfb[k]])
                    for (lo, half, co, so) in rot:
                        x1 = x3[:, :, lo:lo + half]
                        x2 = x3[:, :, lo + half:lo + 2 * half]
                        cs = trig[k][:, co:co + half].unsqueeze(1).to_broadcast([128, nh, half])
                        sn = trig[k][:, so:so + half].unsqueeze(1).to_broadcast([128, nh, half])
                        t = [r_[:, 0:nh, 0:half] for r_ in rt]
                        self.rotary(x1, x2, cs, sn, t, [xfb[k], trigb[k]], xfb[k], rtb)
                    S.op("act", lambda e, k=k, n=n: e.copy(out=xbf[k][:, 0:n], in_=xf[k][:, 0:n]),
                         reads=[xfb[k]], writes=[xbfb[k]])
                    if dil == 1:
                        self.transposes(xbf[k], xbfb[k], nh,
                                        lambda i, kk, ct=ct, stg=stg: stg[:, i:i + kk, ct * 128:(ct + 1) * 128],
                                        [stgb], pst, pstb, cnt)
                    else:
                        w_ = 128 // dil
                        self.transposes(xbf[k], xbfb[k], nh,
                                        lambda i, kk, ct=ct, stg=stg, w_=w_, dil=dil:
                                        stg[:, i, :].rearrange("p (r l) -> p r l", r=dil)[:, :, w_ * ct:w_ * (ct + 1)],
                                        [stgb], pst, pstb, cnt, perm=dil)
                for i in range(nh):
                    self.dma("sp", dstT[h0 + i], stg[:, i, :], reads=[stgb])

            def v_group(c0, n, dil, dst):
                wt, wtb = load_w(c0, n)
                for ct in range(NT):
                    p, pb = mm_tokmajor(wt, wtb, n, 1, ct)
                    k = tilec[0] % 2
                    tilec[0] += 1
                    S.op("act", lambda e, p=p, k=k, n=n: e.copy(out=xbf[k][:, 0:n], in_=p[:, 0:n]),
                         reads=[pb], writes=[xbfb[k]])
                    if dil == 1:
                        self.dma("sp", dst[ct * 128:(ct + 1) * 128, 0:n], xbf[k][:, 0:n], reads=[xbfb[k]])
                    else:
                        w_ = 128 // dil
                        L = NTOK // dil
                        for r in range(dil):
                            self.dma("sp", dst[r * L + w_ * ct:r * L + w_ * (ct + 1), 0:n], xbf[k][r:128:dil, 0:n],
                                     reads=[xbfb[k]])

            tg = W["tab_gqa"]
            rot_g = [(0, 32, 0, 32), (64, 32, 64, 96)]
            if P is None or "B" in P:
                head_group(BQ0, 4, "gqa_q_norm", 1, rot_g, tg, 128, self.qT_g, 0)
                head_group(BQ0 + 512, 4, "gqa_q_norm", 1, rot_g, tg, 128, self.qT_g, 4)
                head_group(BK0, 2, "gqa_k_norm", 1, rot_g, tg, 128, self.kT_g, 0)
                v_group(BV0, 256, 1, self.v_g)
            rot_d = [(0, 16, 0, 16)]
            for g, dil in enumerate(DILS):
                if P is not None and ("C%d" % g) not in P:
                    continue
                td = W["tab_dil"][0]
                head_group(CQD0 + g * 512, 4, "dil_q_norm", dil, rot_d, td, 32, self.qT_d, g * 4)
                head_group(CKD0 + g * 512, 4, "dil_k_norm", dil, rot_d, td, 32, self.kT_d, g * 4)
                v_group(CVD0 + g * 512, 512, dil, self.v_d[g])

            ngate = 12 if (P is None or "G" in P) else 0
            for c in range(48):
                self.dma("sp", bias[:, c:c + 1], W["b_gate"][l][c * 128:(c + 1) * 128].rearrange("(p o) -> p o", o=1),
                         writes=[bb])
            for wg in range(ngate):
                wt, wtb = load_w(G0 + wg * 512, 512)
                for j in range(4):
                    c = wg * 4 + j
                    si = c % 2
                    for tb in range(4):
                        i = mmc[0] % 3
                        mmc[0] += 1
                        for dc in range(16):
                            S.op("pe", lambda e, i=i, dc=dc, wt=wt, j=j, tb=tb:
                                 e.matmul(psm[i], lhsT=wt[:, dc, j * 128:(j + 1) * 128],
                                          rhs=self.XT[:, dc, tb * 512:(tb + 1) * 512],
                                          start=(dc == 0), stop=(dc == 15)),
                                 reads=self.XTB[4 * tb:4 * tb + 4] + [wtb], writes=[psmb[i]])
                        S.op("act", lambda e, i=i, si=si, tb=tb, c=c: e.activation(
                            out=sg[si][:, tb * 512:(tb + 1) * 512], in_=psm[i], func=AF.Sigmoid, bias=bias[:, c:c + 1]),
                            reads=[psmb[i], bb], writes=[sgb[si]])
                    self.dma("sp", self.sigT[c], sg[si], reads=[sgb[si]])
        S.barrier()

    def phase_mla2(self, l):
        S, W = self.S, self.W
        with ExitStack() as st:
            wuq = self.sb(st, [128, 4, 1536], BF16, "wuq")
            wukv = self.sb(st, [128, 2, 2048], BF16, "wukv")
            wb_ = Buf()
            self.wload(wuq, W["w_uq"][l], wb_, nsplit=2)
            self.wload(wukv, W["w_ukv"][l], wb_, nsplit=2)
            gq = self.sb(st, [128, 192], F32, "gq")
            gk = self.sb(st, [128, 192], F32, "gk")
            gb = Buf()
            self.dma("sp", gq, W["mla_q_head_norm"][l].partition_broadcast(128), writes=[gb])
            self.dma("sp", gk, W["mla_k_head_norm"][l].partition_broadcast(128), writes=[gb])
            psm = [self.ps(st, [128, 512], F32) for _ in range(4)]
            psmb = [Buf() for _ in range(4)]
            pst = [self.ps(st, [128, 4, 128], BF16) for _ in range(4)]
            pstb = [Buf() for _ in range(4)]
            qf = self.sb(st, [128, 1536], F32, "qf")
            qfb = Buf()
            kvf = self.sb(st, [128, 2048], F32, "kvf")
            kvfb = Buf()
            sq = self.sb(st, [128, 1536], F32, "sq")
            sqb = Buf()
            st16 = self.sb(st, [128, 40], F32, "st16")
            stb = Buf()
            trig = [self.sb(st, [128, 64], F32, "trig") for _ in range(2)]
            trigb = [Buf() for _ in range(2)]
            rt = [self.sb(st, [128, 8, 32], F32, "rt") for _ in range(4)]
            rtb = Buf()
            R = self.sb(st, [128, 64], F32, "R")
            Rb = Buf()
            qb = self.sb(st, [128, 8, 192], BF16, "qb")
            qbb = Buf()
            kb = self.sb(st, [128, 8, 128], BF16, "kb")
            kbb = Buf()
            krb = self.sb(st, [128, 8, 64], BF16, "krb")
            krbb = Buf()
            vb = [self.sb(st, [128, 8, 128], BF16, "vb") for _ in range(2)]
            vbb = [Buf() for _ in range(2)]
            sqn = self.sb(st, [128, 8, 512], BF16, "sqn")
            sqr = self.sb(st, [128, 8, 512], BF16, "sqr")
            skn = self.sb(st, [128, 8, 512], BF16, "skn")
            skr = self.sb(st, [128, 8, 512], BF16, "skr")
            sgb_ = [Buf() for _ in range(4)]
            cnt = [0]
            tm = W["tab_mla"]
            for tt in range(NT):
                k2 = tt % 2
                tsl = slice(tt * 128, (tt + 1) * 128)
                ssl = slice((tt % 4) * 128, (tt % 4 + 1) * 128)
                self.dma("sp", trig[k2], tm[tsl, :], writes=[trigb[k2]])
                cs8 = trig[k2][:, 0:32].unsqueeze(1).to_broadcast([128, 8, 32])
                sn8 = trig[k2][:, 32:64].unsqueeze(1).to_broadcast([128, 8, 32])
                for g in range(3):
                    for j in range(4):
                        S.op("pe", lambda e, g=g, j=j, tsl=tsl: e.matmul(
                            psm[g], lhsT=self.CQT[:, j, tsl], rhs=wuq[:, j, g * 512:(g + 1) * 512],
                            start=(j == 0), stop=(j == 3)), reads=[self.cqb, wb_], writes=[psmb[g]])
                    S.op("act", lambda e, g=g: e.copy(out=qf[:, g * 512:(g + 1) * 512], in_=psm[g]),
                         reads=[psmb[g]], writes=[qfb])
                q3 = qf.rearrange("p (h d) -> p h d", d=192)
                S.op("dve", lambda e: e.tensor_tensor(out=sq, in0=qf, in1=qf, op=ALU.mult), reads=[qfb], writes=[sqb])
                S.op("dve", lambda e: e.tensor_reduce(out=st16[:, 0:8], in_=sq.rearrange("p (h d) -> p h d", d=192),
                                                      axis=AX.X, op=ALU.add), reads=[sqb], writes=[stb])
                self.rstd(st16[:, 0:8], st16[:, 8:16], 192, stb)
                S.op("dve", lambda e, q3=q3: e.tensor_tensor(
                    out=q3, in0=q3, in1=st16[:, 0:8].unsqueeze(2).to_broadcast([128, 8, 192]), op=ALU.mult),
                    reads=[qfb, stb], writes=[qfb])
                S.op("pool", lambda e, q3=q3: e.tensor_tensor(
                    out=q3, in0=q3, in1=gq.unsqueeze(1).to_broadcast([128, 8, 192]), op=ALU.mult),
                    reads=[qfb, gb], writes=[qfb])
                self.rotary(q3[:, :, 128:160], q3[:, :, 160:192], cs8, sn8, rt, [qfb, trigb[k2]], qfb, rtb)
                S.op("act", lambda e, q3=q3: e.copy(out=qb, in_=q3), reads=[qfb], writes=[qbb])
                self.transposes(None, qbb, 8, lambda i, k, ssl=ssl: sqn[:, i:i + k, ssl], [sgb_[0]], pst, pstb, cnt,
                                src_fn=lambda i: qb[:, i, 0:128])
                self.transposes(None, qbb, 8, lambda i, k, ssl=ssl: sqr[0:64, i:i + k, ssl], [sgb_[1]], pst, pstb, cnt,
                                rows=64, src_fn=lambda i: qb[:, i, 128:192])
                for g in range(4):
                    for j in range(2):
                        S.op("pe", lambda e, g=g, j=j, tsl=tsl: e.matmul(
                            psm[g], lhsT=self.CKVT[:, j, tsl], rhs=wukv[:, j, g * 512:(g + 1) * 512],
                            start=(j == 0), stop=(j == 1)), reads=[self.ckvb, wb_], writes=[psmb[g]])
                    S.op("act", lambda e, g=g: e.copy(out=kvf[:, g * 512:(g + 1) * 512], in_=psm[g]),
                         reads=[psmb[g]], writes=[kvfb])
                kv3 = kvf.rearrange("p (h d) -> p h d", d=256)
                kn3 = kv3[:, :, 0:128]
                v3 = kv3[:, :, 128:256]
                sq3 = sq[:, 0:1024].rearrange("p (h d) -> p h d", d=128)
                S.op("dve", lambda e, kn3=kn3, sq3=sq3: e.tensor_tensor(out=sq3, in0=kn3, in1=kn3, op=ALU.mult),
                     reads=[kvfb], writes=[sqb])
                S.op("dve", lambda e, sq3=sq3: e.tensor_reduce(out=st16[:, 16:24], in_=sq3, axis=AX.X, op=ALU.add),
                     reads=[sqb], writes=[stb])
                kpe = self.KPE[:, tt, :]
                S.op("act", lambda e, kpe=kpe: e.activation(out=R, in_=kpe, func=AF.Square, accum_out=st16[:, 32:33]),
                     reads=[self.kpeb], writes=[Rb, stb])
                S.op("dve", lambda e: e.tensor_scalar(out=st16[:, 16:24], in0=st16[:, 16:24], scalar1=st16[:, 32:33],
                                                      scalar2=None, op0=ALU.add), reads=[stb], writes=[stb])
                self.rstd(st16[:, 16:24], st16[:, 24:32], 192, stb)
                S.op("dve", lambda e, kn3=kn3: e.tensor_tensor(
                    out=kn3, in0=kn3, in1=st16[:, 16:24].unsqueeze(2).to_broadcast([128, 8, 128]), op=ALU.mult),
                    reads=[kvfb, stb], writes=[kvfb])
                S.op("pool", lambda e, kn3=kn3: e.tensor_tensor(
                    out=kn3, in0=kn3, in1=gk[:, 0:128].unsqueeze(1).to_broadcast([128, 8, 128]), op=ALU.mult),
                    reads=[kvfb, gb], writes=[kvfb])
                S.op("act", lambda e, kn3=kn3: e.copy(out=kb, in_=kn3), reads=[kvfb], writes=[kbb])
                S.op("act", lambda e, v3=v3, k2=k2: e.copy(out=vb[k2], in_=v3), reads=[kvfb], writes=[vbb[k2]])
                self.dma("sp", self.v_mla[tsl, :].rearrange("p (h d) -> p h d", d=128), vb[k2], reads=[vbb[k2]])
                S.op("pool", lambda e, kpe=kpe: e.tensor_tensor(out=R, in0=kpe, in1=gk[:, 128:192], op=ALU.mult),
                     reads=[self.kpeb, gb, Rb], writes=[Rb])
                t1 = [r_[:, 0, :] for r_ in rt]
                self.rotary(R[:, 0:32], R[:, 32:64], trig[k2][:, 0:32], trig[k2][:, 32:64], t1, [Rb, trigb[k2]], Rb, rtb)
                S.op("dve", lambda e: e.tensor_tensor(
                    out=krb, in0=R.unsqueeze(1).to_broadcast([128, 8, 64]),
                    in1=st16[:, 16:24].unsqueeze(2).to_broadcast([128, 8, 64]), op=ALU.mult),
                    reads=[Rb, stb], writes=[krbb])
                self.transposes(None, kbb, 8, lambda i, k, ssl=ssl: skn[:, i:i + k, ssl], [sgb_[2]], pst, pstb, cnt,
                                src_fn=lambda i: kb[:, i, :])
                self.transposes(None, krbb, 8, lambda i, k, ssl=ssl: skr[0:64, i:i + k, ssl], [sgb_[3]], pst, pstb, cnt,
                                rows=64, src_fn=lambda i: krb[:, i, :])
                if tt % 4 == 3:
                    bsl = slice((tt // 4) * 512, (tt // 4 + 1) * 512)
                    self.dma("sp", self.qT_mla[:, 0:128, bsl].rearrange("h p t -> p h t"), sqn, reads=[sgb_[0]])
                    self.dma("sp", self.qT_mla[:, 128:192, bsl].rearrange("h p t -> p h t"), sqr[0:64], reads=[sgb_[1]])
                    self.dma("sp", self.kT_mla[:, 0:128, bsl].rearrange("h p t -> p h t"), skn, reads=[sgb_[2]])
                    self.dma("sp", self.kT_mla[:, 128:192, bsl].rearrange("h p t -> p h t"), skr[0:64], reads=[sgb_[3]])
        S.barrier()

    def phase_attn_full(self, heads):
        S = self.S
        with ExitStack() as st:
            ops_ = []
            for _ in range(2):
                ops_.append(dict(
                    qn=self.sb(st, [128, NTOK], BF16, "qn"), kn=self.sb(st, [128, NTOK], BF16, "kn"),
                    qr=self.sb(st, [128, NTOK], BF16, "qr"), kr=self.sb(st, [128, NTOK], BF16, "kr"),
                    v=self.sb(st, [128, NT, 128], BF16, "v"), buf=Buf()))
            pt = [self.sb(st, [128, 512], BF16, "pt") for _ in range(4)]
            ptb = [Buf() for _ in range(4)]
            osb = [self.sb(st, [128, NTOK], BF16, "osb") for _ in range(2)]
            osbb = [Buf() for _ in range(2)]
            rec = [self.sb(st, [128, 512], F32, "rec") for _ in range(2)]
            recb = [Buf() for _ in range(2)]
            psS = [self.ps(st, [128, 512], F32) for _ in range(4)]
            psSb = [Buf() for _ in range(4)]
            psO = [self.ps(st, [128, 512], F32) for _ in range(2)]
            psOb = [Buf() for _ in range(2)]
            psL = [self.ps(st, [128, 512], F32) for _ in range(2)]
            psLb = [Buf() for _ in range(2)]

            if heads[0]["qr"] is not None:
                for o in ops_:
                    S.op("pool", lambda e, o=o: e.memset(o["qr"], 0.0), writes=[o["buf"]])
                    S.op("pool", lambda e, o=o: e.memset(o["kr"], 0.0), writes=[o["buf"]])

            def load(hi):
                h = heads[hi]
                o = ops_[hi % 2]
                self.dma("sp", o["qn"], h["qn"], writes=[o["buf"]])
                self.dma("sp", o["kn"], h["kn"], writes=[o["buf"]])
                if h["qr"] is not None:
                    self.dma("sp", o["qr"][0:64, :], h["qr"], writes=[o["buf"]])
                    self.dma("sp", o["kr"][0:64, :], h["kr"], writes=[o["buf"]])
                self.dma("sp", o["v"], h["v"].rearrange("(t p) d -> p t d", p=128), writes=[o["buf"]])

            tiles = [(hi, qb, kc) for hi in range(len(heads)) for qb in range(4) for kc in range(NT)]

            def issue_S(n):
                hi, qb, kc = tiles[n]
                h = heads[hi]
                o = ops_[hi % 2]
                si = n % 4
                ks = slice(kc * 128, (kc + 1) * 128)
                qs = slice(qb * 512, (qb + 1) * 512)
                rope = h["qr"] is not None
                S.op("pe", lambda e: e.matmul(psS[si], lhsT=o["kn"][:, ks], rhs=o["qn"][:, qs], start=True, stop=not rope),
                     reads=[o["buf"]], writes=[psSb[si]])
                if rope:
                    S.op("pe", lambda e: e.matmul(psS[si], lhsT=o["kr"][:, ks], rhs=o["qr"][:, qs], start=False, stop=True),
                         reads=[o["buf"]], writes=[psSb[si]])
                S.op("act", lambda e: e.activation(out=pt[si], in_=psS[si], func=AF.Exp, scale=h["scale"]),
                     reads=[psSb[si]], writes=[ptb[si]])

            def issue_PV(n):
                hi, qb, kc = tiles[n]
                h = heads[hi]
                o = ops_[hi % 2]
                si = n % 4
                ob = (n // NT) % 2
                S.op("pe", lambda e: e.matmul(psO[ob], lhsT=o["v"][:, kc, :], rhs=pt[si], start=(kc == 0), stop=(kc == NT - 1)),
                     reads=[o["buf"], ptb[si]], writes=[psOb[ob]])
                S.op("pe", lambda e: e.matmul(psL[ob], lhsT=self.ones, rhs=pt[si], start=(kc == 0), stop=(kc == NT - 1)),
                     reads=[self.cb, ptb[si]], writes=[psLb[ob]])
                if kc == NT - 1:
                    hb = hi % 2
                    S.op("dve", lambda e: e.reciprocal(out=rec[ob], in_=psL[ob]), reads=[psLb[ob]], writes=[recb[ob]])
                    S.op("dve", lambda e: e.tensor_tensor(out=osb[hb][:, qb * 512:(qb + 1) * 512], in0=psO[ob], in1=rec[ob],
                                                          op=ALU.mult), reads=[psOb[ob], recb[ob]], writes=[osbb[hb]])
                    if qb == 3:
                        self.dma("sp", self.oT[h["oblk"]], osb[hb], reads=[osbb[hb]])
                        if hi + 2 < len(heads):
                            load(hi + 2)

            load(0)
            if len(heads) > 1:
                load(1)
            LAG = 2
            for n in range(len(tiles)):
                issue_S(n)
                if n >= LAG:
                    issue_PV(n - LAG)
            for n in range(len(tiles) - LAG, len(tiles)):
                issue_PV(n)
        S.barrier()

    def mla_heads(self):
        return [dict(qn=self.qT_mla[h, 0:128, :], qr=self.qT_mla[h, 128:192, :], kn=self.kT_mla[h, 0:128, :],
                     kr=self.kT_mla[h, 128:192, :], v=self.v_mla[:, h * 128:(h + 1) * 128],
                     scale=float(192 ** -0.5), oblk=h) for h in range(8)]

    def gqa_heads(self):
        return [dict(qn=self.qT_g[h], qr=None, kn=self.kT_g[h // 4], kr=None,
                     v=self.v_g[:, (h // 4) * 128:(h // 4 + 1) * 128], scale=float(128 ** -0.5), oblk=8 + h)
                for h in range(8)]

    def phase_attn_dil(self):
        S = self.S
        scale = float(128 ** -0.5)
        with ExitStack() as st:
            ops_ = []
            for _ in range(2):
                ops_.append(dict(q=self.sb(st, [128, NTOK], BF16, "dq"), k=self.sb(st, [128, NTOK], BF16, "dk"),
                                 v=self.sb(st, [128, NT, 128], BF16, "dv"), vs=self.sb(st, [128, NT - 1, 128], BF16, "dvs"),
                                 buf=Buf()))
            pt = [self.sb(st, [128, 128], BF16, "dpt") for _ in range(4)]
            ptb = [Buf() for _ in range(4)]
            ptm = [self.sb(st, [128, 128], BF16, "dptm") for _ in range(4)]
            ptmb = [Buf() for _ in range(4)]
            Oacc = self.sb(st, [128, NTOK], F32, "Oacc")
            Sacc = self.sb(st, [128, NTOK], F32, "Sacc")
            accb = Buf()
            rec = self.sb(st, [128, NTOK], F32, "drec")
            osb = self.sb(st, [128, NTOK], BF16, "dosb")
            osbb = Buf()
            psS = [self.ps(st, [128, 128], F32) for _ in range(4)]
            psSb = [Buf() for _ in range(4)]
            psO = [self.ps(st, [128, 512], F32) for _ in range(2)]
            psOb = [Buf() for _ in range(2)]
            psL = [self.ps(st, [128, 512], F32) for _ in range(2)]
            psLb = [Buf() for _ in range(2)]

            units = [(hg, g) for hg in range(4) for g in range(3)]

            def load(ui):
                hg, g = units[ui]
                o = ops_[ui % 2]
                hd = g * 4 + hg
                self.dma("sp", o["q"], self.qT_d[hd], writes=[o["buf"]])
                self.dma("sp", o["k"], self.kT_d[hd], writes=[o["buf"]])
                vsrc = self.v_d[g][:, hg * 128:(hg + 1) * 128]
                self.dma("sp", o["v"], vsrc.rearrange("(t p) d -> p t d", p=128), writes=[o["buf"]])
                self.dma("sp", o["vs"], vsrc[64:64 + (NT - 1) * 128, :].rearrange("(t p) d -> p t d", p=128),
                         writes=[o["buf"]])

            tiles = []
            for ui, (hg, g) in enumerate(units):
                dil = DILS[g]
                nb = NT // dil
                L = NTOK // dil
                for cb in range(NT):
                    r, b = cb // nb, cb % nb
                    base = r * L
                    bt = base // 128
                    if nb == 1:
                        ch = [(base, 4, "v", bt)]
                    else:
                        c1 = (base, 2, "v", bt) if b == 0 else (base + 128 * b - 64, 0, "vs", bt + b - 1)
                        c2 = (base + 128 * b, 3, "v", bt + b) if b == nb - 1 else (base + 128 * b + 64, 1, "vs", bt + b)
                        ch = [c1, c2]
                    for ci, c in enumerate(ch):
                        tiles.append((ui, cb, ci, len(ch)) + c)

            bank_of = {}
            nbank = [0]

            def issue_S(n):
                ui, cb, ci, nch, kbase, mid, vkey, vtile = tiles[n]
                o = ops_[ui % 2]
                si = n % 4
                S.op("pe", lambda e: e.matmul(psS[si], lhsT=o["k"][:, kbase:kbase + 128],
                                              rhs=o["q"][:, cb * 128:(cb + 1) * 128], start=True, stop=True),
                     reads=[o["buf"]], writes=[psSb[si]])
                S.op("act", lambda e: e.activation(out=pt[si], in_=psS[si], func=AF.Exp, scale=scale),
                     reads=[psSb[si]], writes=[ptb[si]])
                S.op("pool", lambda e: e.tensor_tensor(out=ptm[si], in0=pt[si], in1=self.mask[:, mid, :], op=ALU.mult),
                     reads=[ptb[si], self.cb], writes=[ptmb[si]])

            def issue_PV(n):
                ui, cb, ci, nch, kbase, mid, vkey, vtile = tiles[n]
                hg, g = units[ui]
                dil = DILS[g]
                o = ops_[ui % 2]
                si = n % 4
                key = (ui, cb // 4)
                if key not in bank_of:
                    bank_of[key] = nbank[0] % 2
                    nbank[0] += 1
                ob = bank_of[key]
                cs_ = slice((cb % 4) * 128, (cb % 4 + 1) * 128)
                S.op("pe", lambda e: e.matmul(psO[ob][:, cs_], lhsT=o[vkey][:, vtile, :], rhs=ptm[si],
                                              start=(ci == 0), stop=(ci == nch - 1), skip_group_check=True),
                     reads=[o["buf"], ptmb[si]], writes=[psOb[ob]])
                S.op("pe", lambda e: e.matmul(psL[ob][:, cs_], lhsT=self.ones, rhs=ptm[si],
                                              start=(ci == 0), stop=(ci == nch - 1), skip_group_check=True),
                     reads=[self.cb, ptmb[si]], writes=[psLb[ob]])
                if cb % 4 == 3 and ci == nch - 1:
                    nb = NT // dil
                    if nb >= 4:
                        r, b = cb // nb, cb % nb
                        m = b // 4
                        start = r + dil * 512 * m
                        osl = slice(start, start + dil * 511 + 1, dil)
                        ov, sv = Oacc[:, osl], Sacc[:, osl]
                        po, pl = psO[ob], psL[ob]
                    else:
                        r0 = cb - 3
                        ov = Oacc.rearrange("p (l r) -> p r l", r=16)[:, r0:r0 + 4, :]
                        sv = Sacc.rearrange("p (l r) -> p r l", r=16)[:, r0:r0 + 4, :]
                        po = psO[ob].rearrange("p (c l) -> p c l", l=128)
                        pl = psL[ob].rearrange("p (c l) -> p c l", l=128)
                    if g == 0:
                        S.op("act", lambda e: e.copy(out=ov, in_=po), reads=[psOb[ob]], writes=[accb])
                        S.op("act", lambda e: e.copy(out=sv, in_=pl), reads=[psLb[ob]], writes=[accb])
                    else:
                        S.op("dve", lambda e: e.tensor_tensor(out=ov, in0=po, in1=ov, op=ALU.add),
                             reads=[psOb[ob], accb], writes=[accb])
                        S.op("dve", lambda e: e.tensor_tensor(out=sv, in0=pl, in1=sv, op=ALU.add),
                             reads=[psLb[ob], accb], writes=[accb])
                    if g == 2 and cb == NT - 1:
                        S.op("dve", lambda e: e.reciprocal(out=rec, in_=Sacc), reads=[accb], writes=[accb])
                        S.op("dve", lambda e: e.tensor_tensor(out=osb, in0=Oacc, in1=rec, op=ALU.mult),
                             reads=[accb, osbb], writes=[osbb])
                        self.dma("sp", self.oT[16 + hg], osb, reads=[osbb])
                    if cb == NT - 1 and ui + 2 < len(units):
                        load(ui + 2)

            load(0)
            load(1)
            LAG = 2
            for n in range(len(tiles)):
                issue_S(n)
                if n >= LAG:
                    issue_PV(n - LAG)
            for n in range(len(tiles) - LAG, len(tiles)):
                issue_PV(n)
        S.barrier()

    def phase_merge(self, l):
        S, W = self.S, self.W
        with ExitStack() as st:
            OT = self.sb(st, [128, 20, NTOK], BF16, "OT")
            otb = Buf()
            for i in range(20):
                self.dma("sp", OT[:, i, :], self.oT[i], writes=[otb])
            wo = [self.sb(st, [128, 20, 256], BF16, "wo") for _ in range(2)]
            wob = [Buf() for _ in range(2)]
            sgt = [[self.sb(st, [128, 512], F32, "sgt") for _ in range(3)] for _ in range(2)]
            sgtb = [Buf() for _ in range(2)]
            m = [[self.sb(st, [128, 512], F32, "m") for _ in range(3)] for _ in range(2)]
            mb = [Buf() for _ in range(2)]
            psY = [[self.ps(st, [128, 512], F32) for _ in range(3)] for _ in range(2)]
            psYb = [[Buf() for _ in range(3)] for _ in range(2)]
            n = 0
            for cg in range(8):
                w = wo[cg % 2]
                wb_ = wob[cg % 2]
                c0 = cg * 256
                self.wload(w[:, 0:8, :], W["w_oa"][l][:, c0:c0 + 256], wb_, nsplit=1)
                self.wload(w[:, 8:16, :], W["w_ob"][l][:, c0:c0 + 256], wb_, nsplit=1)
                self.wload(w[:, 16:20, :], W["w_oc"][l][:, c0:c0 + 256], wb_, nsplit=1)
                for j in range(2):
                    c = cg * 2 + j
                    for tb in range(4):
                        s_ = n % 2
                        n += 1
                        tsl = slice(tb * 512, (tb + 1) * 512)
                        for i in range(3):
                            self.dma("sp", sgt[s_][i], self.sigT[i * 16 + c][:, tsl], writes=[sgtb[s_]])
                        for (bi, lo, hi) in ((0, 0, 8), (1, 8, 16), (2, 16, 20)):
                            for i in range(lo, hi):
                                S.op("pe", lambda e, s_=s_, bi=bi, i=i, lo=lo, hi=hi, w=w, j=j, tsl=tsl: e.matmul(
                                    psY[s_][bi], lhsT=w[:, i, j * 128:(j + 1) * 128], rhs=OT[:, i, tsl],
                                    start=(i == lo), stop=(i == hi - 1)), reads=[otb, wb_], writes=[psYb[s_][bi]])
                        for bi in range(3):
                            S.op("dve", lambda e, s_=s_, bi=bi: e.tensor_tensor(
                                out=m[s_][bi], in0=psY[s_][bi], in1=sgt[s_][bi], op=ALU.mult),
                                reads=[psYb[s_][bi], sgtb[s_]], writes=[mb[s_]])
                        S.op("pool", lambda e, s_=s_: e.tensor_tensor(out=m[s_][0], in0=m[s_][0], in1=m[s_][1], op=ALU.add),
                             reads=[mb[s_]], writes=[mb[s_]])
                        S.op("pool", lambda e, s_=s_, c=c, tsl=tsl: e.tensor_tensor(
                            out=self.XT[:, c, tsl], in0=m[s_][0], in1=m[s_][2], op=ALU.add),
                            reads=[mb[s_]], writes=self.XTB[4 * tb:4 * tb + 4])
        S.barrier()

    def phase_outproj(self, l, x_cur, x1_dst):
        S, W = self.S, self.W
        with ExitStack() as st:
            wout = self.sb(st, [128, 16, DM], BF16, "wout")
            wb_ = Buf()
            self.wload(wout, W["w_out"][l], wb_, nsplit=16)
            xt = self.sb(st, [128, DM], F32, "xt")
            xtb = Buf()
            x1 = [self.sb(st, [128, DM], F32, "x1") for _ in range(2)]
            x1b = [Buf() for _ in range(2)]
            hn = self.sb(st, [128, DM], BF16, "hn")
            hnb = Buf()
            junk = self.sb(st, [128, DM], BF16, "junk")
            jb = Buf()
            gbc = self.sb(st, [128, DM], F32, "gbc")
            gb = Buf()
            st4 = [self.sb(st, [128, 4], F32) for _ in range(2)]
            stb = [Buf() for _ in range(2)]
            psm = [self.ps(st, [128, 512], F32) for _ in range(5)]
            psmb = [Buf() for _ in range(5)]
            pst = [self.ps(st, [128, 4, 128], BF16) for _ in range(3)]
            pstb = [Buf() for _ in range(3)]
            cnt = [0]
            self.dma("sp", gbc, W["mlp_norm"][l].partition_broadcast(128), writes=[gb])
            n = 0
            for tt in range(NT):
                b = tt % 2
                tsl = slice(tt * 128, (tt + 1) * 128)
                self.dma("sp", xt, x_cur[tsl, :], writes=[xtb])
                for cg in range(4):
                    i = n % 5
                    n += 1
                    csl = slice(cg * 512, (cg + 1) * 512)
                    for dc in range(16):
                        S.op("pe", lambda e, i=i, dc=dc, tsl=tsl, csl=csl: e.matmul(
                            psm[i], lhsT=self.XT[:, dc, tsl], rhs=wout[:, dc, csl], start=(dc == 0), stop=(dc == 15)),
                            reads=[self.XTB[tt], wb_], writes=[psmb[i]])
                    S.op("dve", lambda e, i=i, b=b, csl=csl: e.tensor_tensor(out=x1[b][:, csl], in0=psm[i], in1=xt[:, csl],
                                                                            op=ALU.add),
                         reads=[psmb[i], xtb], writes=[x1b[b]])
                self.dma("sp", x1_dst[tsl, :], x1[b], reads=[x1b[b]])
                self.norm_tile(x1[b], x1b[b], hn, hnb, junk, jb, gbc, gb, st4[b], stb[b], DM)
                self.transposes(hn, hnb, 16, lambda i, k, tsl=tsl: self.XT[:, i:i + k, tsl], [self.XTB[tt]], pst, pstb, cnt)
        S.barrier()

    def phase_ffn(self, l, x1_src, x_dst):
        S, W = self.S, self.W
        with ExitStack() as st:
            hT = self.sb(st, [128, 64, 512], BF16, "hT")
            hTb = [Buf() for _ in range(16)]
            wup = [self.sb(st, [128, 16, 512], BF16, "wup") for _ in range(2)]
            wupb = [Buf() for _ in range(2)]
            wdn = [self.sb(st, [128, 8, 512], BF16, "wdn") for _ in range(3)]
            wdnb = [Buf() for _ in range(3)]
            r32 = [self.sb(st, [128, 512], F32, "r32") for _ in range(2)]
            r32b = [Buf() for _ in range(2)]
            x1t = [self.sb(st, [128, 512], F32, "x1t") for _ in range(2)]
            x1tb = [Buf() for _ in range(2)]
            x2t = [self.sb(st, [128, 512], F32, "x2t") for _ in range(2)]
            x2tb = [Buf() for _ in range(2)]
            psU = [self.ps(st, [128, 512], F32) for _ in range(3)]
            psUb = [Buf() for _ in range(3)]
            psD = [self.ps(st, [128, 512], F32) for _ in range(4)]
            psDb = [Buf() for _ in range(4)]
            nu = 0
            nw = 0
            nd = 0
            nx = 0
            for tb in range(4):
                tsl = slice(tb * 512, (tb + 1) * 512)
                for fg in range(16):
                    w, wb_ = wup[nw % 2], wupb[nw % 2]
                    nw += 1
                    self.wload(w, W["w_up"][l][:, fg * 512:(fg + 1) * 512], wb_, nsplit=4)
                    for j in range(4):
                        i = nu % 3
                        k = nu % 2
                        nu += 1
                        for dc in range(16):
                            S.op("pe", lambda e, i=i, dc=dc, w=w, j=j, tsl=tsl: e.matmul(
                                psU[i], lhsT=w[:, dc, j * 128:(j + 1) * 128], rhs=self.XT[:, dc, tsl],
                                start=(dc == 0), stop=(dc == 15)),
                                reads=self.XTB[4 * tb:4 * tb + 4] + [wb_], writes=[psUb[i]])
                        S.op("act", lambda e, i=i, k=k: e.activation(out=r32[k], in_=psU[i], func=AF.Relu),
                             reads=[psUb[i]], writes=[r32b[k]])
                        S.op("pool", lambda e, k=k, fc=fg * 4 + j: e.tensor_tensor(out=hT[:, fc, :], in0=r32[k], in1=r32[k],
                                                                                 op=ALU.mult),
                             reads=[r32b[k]], writes=[hTb[fg]])
                for cg in range(4):
                    csl = slice(cg * 512, (cg + 1) * 512)
                    for ks in range(8):
                        w, wb_ = wdn[nd % 3], wdnb[nd % 3]
                        nd += 1
                        self.wload(w, W["w_down"][l][ks * 1024:(ks + 1) * 1024, csl], wb_, nsplit=2)
                        for t4 in range(4):
                            for i in range(8):
                                S.op("pe", lambda e, t4=t4, i=i, ks=ks, w=w: e.matmul(
                                    psD[t4], lhsT=hT[:, ks * 8 + i, t4 * 128:(t4 + 1) * 128], rhs=w[:, i, :],
                                    start=(ks == 0 and i == 0), stop=(ks == 7 and i == 7)),
                                    reads=[hTb[2 * ks], hTb[2 * ks + 1], wb_], writes=[psDb[t4]])
                    for t4 in range(4):
                        k = nx % 2
                        nx += 1
                        rsl = slice((tb * 4 + t4) * 128, (tb * 4 + t4 + 1) * 128)
                        self.dma("sp", x1t[k], x1_src[rsl, csl], writes=[x1tb[k]])
                        S.op("dve", lambda e, t4=t4, k=k: e.tensor_tensor(out=x2t[k], in0=psD[t4], in1=x1t[k], op=ALU.add),
                             reads=[psDb[t4], x1tb[k]], writes=[x2tb[k]])
                        self.dma("sp", x_dst[rsl, csl], x2t[k], reads=[x2tb[k]])
        S.barrier()

    def build(self):
        S = self.S
        order = ["consts", "normt", "proj", "mla2", "attn_mla", "attn_gqa", "attn_dil", "merge", "outproj", "ffn"]
        stop = self.stop_after

        def done(name):
            return stop is not None and order.index(name) >= order.index(stop)
        with ExitStack() as gst:
            self.setup_consts(gst)
            S.barrier()
            for l in range(self.nlayers):
                if done("consts"):
                    break
                x_cur = self.x_in if l == 0 else self.xa
                x_next = self.xa if l < DEPTH - 1 else self.out
                self.phase_normt(x_cur, self.W["attn_norm"][l])
                if done("normt"):
                    break
                with ExitStack() as mst:
                    self.CQT = self.sb(mst, [128, 4, NTOK], BF16, "CQT")
                    self.CKVT = self.sb(mst, [128, 2, NTOK], BF16, "CKVT")
                    self.KPE = self.sb(mst, [128, NT, 64], F32, "KPE")
                    self.cqb, self.ckvb, self.kpeb = Buf(), Buf(), Buf()
                    self.phase_proj(l)
                    if not done("proj"):
                        self.phase_mla2(l)
                if done("mla2"):
                    break
                self.phase_attn_full(self.mla_heads())
                if done("attn_mla"):
                    break
                self.phase_attn_full(self.gqa_heads())
                if done("attn_gqa"):
                    break
                self.phase_attn_dil()
                if done("attn_dil"):
                    break
                self.phase_merge(l)
                if done("merge"):
                    break
                self.phase_outproj(l, x_cur, self.xb)
                if done("outproj"):
                    break
                self.phase_ffn(l, self.xb, x_next)
            if self.debug and hasattr(self, "dbg_xt"):
                self.dma("sp", self.dbg_xt, self.XT.rearrange("p c t -> p (c t)"), reads=self.XTB)
            stats = S.emit()
        return stats


_CACHE = {}


def kernel(**inputs):
    if "b" not in _CACHE:
        b = Builder()
        b.build()
        _CACHE["b"] = b
    b = _CACHE["b"]
    tabs = make_tables()
    base = {k: np.ascontiguousarray(np.asarray(inputs[k], dtype=np.float32)) for k in WSHAPES if k in inputs}
    base.update(tabs)
    x = np.asarray(inputs["x"], dtype=np.float32)
    in_maps = [dict(base, x=np.ascontiguousarray(x[i])) for i in range(8)]
    res = run_bass_kernel_spmd(b.nc, in_maps, core_ids=list(range(8)))
    return np.stack([np.asarray(r["out"], dtype=np.float32) for r in res.results], 0)
```
